# Optimizing a Trainium2 kernel written in Bass

```python
import jax, jax.numpy as jnp
from jax import lax
import numpy as np

D_MODEL = 2048
BATCH = 2
SEQ = 8192
DEPTH = 1
DEC_BATCH = 2
DEC_SEQ = 16384
PAST_LEN = 128

MLA_HEADS = 8
Q_LORA = 512
KV_LORA = 512
QK_NOPE = 128
QK_ROPE = 64
V_DIM = 128
DIL_PATTERNS = ((128, 1), (512, 4), (2048, 16))
N_DIL_GROUPS = 3
HEADS_PER_GROUP = 4
DIL_HEAD_DIM = 128
DIL_WIDTH = N_DIL_GROUPS * HEADS_PER_GROUP * DIL_HEAD_DIM
D_FF = 5504
ROPE_THETA = 10000.0
Q_BLOCK = 128
LN_EPS = 1e-5
RMS_EPS = 1e-6
NEG_INF = -1e30
DEEPNORM_ALPHA = (2 * DEPTH) ** 0.25
DEEPNORM_BETA = (8 * DEPTH) ** -0.25
IN_COLS = Q_LORA + KV_LORA + QK_ROPE + 3 * DIL_WIDTH + 2 * D_MODEL

kernel_name = 'hybrid_mla_dilated_macaron_encoder'


def layer_norm(x, g, b):
    xf = x.astype(jnp.float32)
    mu = xf.mean(-1, keepdims=True)
    var = jnp.square(xf - mu).mean(-1, keepdims=True)
    return ((xf - mu) * lax.rsqrt(var + LN_EPS) * g.astype(jnp.float32) + b.astype(jnp.float32)).astype(x.dtype)


def rms_norm(x, g):
    xf = x.astype(jnp.float32)
    return (xf * lax.rsqrt(jnp.square(xf).mean(-1, keepdims=True) + RMS_EPS) * g.astype(jnp.float32)).astype(x.dtype)


def rope(x):
    S, d = x.shape[1], x.shape[-1]
    inv_freq = ROPE_THETA ** (-jnp.arange(0, d, 2, dtype=jnp.float32) / d)
    ang = jnp.arange(S, dtype=jnp.float32)[:, None] * inv_freq[None, :]
    cos = jnp.cos(ang)[None, :, None, :]
    sin = jnp.sin(ang)[None, :, None, :]
    xf = x.astype(jnp.float32)
    x1, x2 = jnp.split(xf, 2, axis=-1)
    return jnp.concatenate([x1 * cos - x2 * sin, x2 * cos + x1 * sin], axis=-1).astype(x.dtype)


def swiglu(x, w_in, w_out):
    g, u = jnp.split(x @ w_in, 2, axis=-1)
    return (jax.nn.silu(g) * u) @ w_out


def mla_attention(q_nope, q_pe, k_nope, k_pe, v):
    B, S, H, _ = q_nope.shape
    nq = S // Q_BLOCK
    scale = (QK_NOPE + QK_ROPE) ** -0.5

    def blocks(t):
        return t.reshape((B, nq, Q_BLOCK) + t.shape[2:]).swapaxes(0, 1)

    def attend(qs):
        qn, qp = qs
        s = (jnp.einsum('bqhd,bkhd->bhqk', qn, k_nope).astype(jnp.float32)
             + jnp.einsum('bqhd,bkd->bhqk', qp, k_pe).astype(jnp.float32)) * scale
        p = jax.nn.softmax(s, axis=-1)
        return jnp.einsum('bhqk,bkhd->bqhd', p.astype(v.dtype), v)

    o = lax.map(attend, (blocks(q_nope), blocks(q_pe)))
    return o.swapaxes(0, 1).reshape(B, S, H * V_DIM)


def banded_attention(q, k, v, half):
    N, L, H, dh = q.shape
    nb = -(-L // half)
    pad = nb * half - L
    qb = jnp.pad(q, ((0, 0), (0, pad), (0, 0), (0, 0))).reshape(N, nb, half, H, dh)

    def windows(t):
        tp = jnp.pad(t, ((0, 0), (half, half + pad), (0, 0), (0, 0))).reshape(N, nb + 2, half, H, dh)
        return jnp.concatenate([tp[:, :-2], tp[:, 1:-1], tp[:, 2:]], axis=2)

    kw, vw = windows(k), windows(v)
    blk = jnp.arange(nb)[:, None] * half
    qpos = blk + jnp.arange(half)[None, :]
    kpos = blk - half + jnp.arange(3 * half)[None, :]
    mask = ((jnp.abs(qpos[:, :, None] - kpos[:, None, :]) <= half)
            & (kpos[:, None, :] >= 0) & (kpos[:, None, :] < L))
    s = jnp.einsum('nbqhd,nbkhd->nbhqk', qb, kw).astype(jnp.float32) * (dh ** -0.5)
    s = jnp.where(mask[None, :, None], s, NEG_INF)
    m = s.max(-1, keepdims=True)
    p = jnp.exp(s - m)
    denom = p.sum(-1, keepdims=True)
    o = jnp.einsum('nbhqk,nbkhd->nbqhd', (p / denom).astype(v.dtype), vw)
    lse = (m + jnp.log(denom))[..., 0]
    o = o.reshape(N, nb * half, H, dh)[:, :L]
    lse = lse.transpose(0, 1, 3, 2).reshape(N, nb * half, H)[:, :L]
    return o, lse


def dilated_group(q, k, v, dilation, half):
    B, S, H, dh = q.shape
    L = S // dilation

    def to_streams(t):
        return t.reshape(B, L, dilation, H, dh).transpose(0, 2, 1, 3, 4).reshape(B * dilation, L, H, dh)

    o, lse = banded_attention(to_streams(q), to_streams(k), to_streams(v), half)
    o = o.reshape(B, dilation, L, H, dh).transpose(0, 2, 1, 3, 4).reshape(B, S, H, dh)
    lse = lse.reshape(B, dilation, L, H).transpose(0, 2, 1, 3).reshape(B, S, H)
    return o, lse


def token_mixer(h, w_in_mix, b_gate, q_norm_g, w_uq, kv_norm_g, w_ukv, w_branch_a, w_branch_b, w_out_mix):
    B, S, _ = h.shape
    z = h @ w_in_mix
    o1 = Q_LORA
    o2 = o1 + KV_LORA
    o3 = o2 + QK_ROPE
    o4 = o3 + 3 * DIL_WIDTH
    c_q, c_kv, k_rope, qkv_d, gate_logits = z[..., :o1], z[..., o1:o2], z[..., o2:o3], z[..., o3:o4], z[..., o4:]

    q = (rms_norm(c_q, q_norm_g) @ w_uq).reshape(B, S, MLA_HEADS, QK_NOPE + QK_ROPE)
    q_nope, q_pe = q[..., :QK_NOPE], rope(q[..., QK_NOPE:])
    kv = (rms_norm(c_kv, kv_norm_g) @ w_ukv).reshape(B, S, MLA_HEADS, QK_NOPE + V_DIM)
    k_nope, v_a = kv[..., :QK_NOPE], kv[..., QK_NOPE:]
    k_pe = rope(k_rope[:, :, None, :])[:, :, 0, :]
    out_a = mla_attention(q_nope, q_pe, k_nope, k_pe, v_a)

    n_h = N_DIL_GROUPS * HEADS_PER_GROUP
    qd, kd, vd = jnp.split(qkv_d, 3, axis=-1)
    qd = rope(qd.reshape(B, S, n_h, DIL_HEAD_DIM))
    kd = rope(kd.reshape(B, S, n_h, DIL_HEAD_DIM))
    vd = vd.reshape(B, S, n_h, DIL_HEAD_DIM)
    outs, lses = [], []
    for g, (window, dilation) in enumerate(DIL_PATTERNS):
        sl = slice(g * HEADS_PER_GROUP, (g + 1) * HEADS_PER_GROUP)
        o_g, l_g = dilated_group(qd[:, :, sl], kd[:, :, sl], vd[:, :, sl], dilation, window // (2 * dilation))
        outs.append(o_g)
        lses.append(l_g)
    wts = jax.nn.softmax(jnp.stack(lses, axis=0), axis=0)
    out_b = jnp.sum(wts[..., None] * jnp.stack(outs, axis=0).astype(jnp.float32), axis=0)
    out_b = out_b.astype(h.dtype).reshape(B, S, HEADS_PER_GROUP * DIL_HEAD_DIM)

    g_a, g_b = jnp.split(jax.nn.sigmoid(gate_logits + b_gate), 2, axis=-1)
    merged = g_a * (out_a @ w_branch_a) + g_b * (out_b @ w_branch_b)
    return merged @ w_out_mix


def encoder_layer(x, ffn1_w_in, ffn1_w_out, ln1_g, ln1_b, w_in_mix, b_gate, q_norm_g, w_uq, kv_norm_g, w_ukv,
                  w_branch_a, w_branch_b, w_out_mix, ln2_g, ln2_b, ffn2_w_in, ffn2_w_out, ln3_g, ln3_b):
    x = layer_norm(DEEPNORM_ALPHA * x + 0.5 * swiglu(x, ffn1_w_in, ffn1_w_out), ln1_g, ln1_b)
    x = layer_norm(DEEPNORM_ALPHA * x + token_mixer(x, w_in_mix, b_gate, q_norm_g, w_uq, kv_norm_g, w_ukv,
                                                    w_branch_a, w_branch_b, w_out_mix), ln2_g, ln2_b)
    x = layer_norm(DEEPNORM_ALPHA * x + 0.5 * swiglu(x, ffn2_w_in, ffn2_w_out), ln3_g, ln3_b)
    return x


def setup_inputs(seed: int = 0) -> dict:
    key = jax.random.key(seed)
    ks = jax.random.split(key, 21)

    def w(k, shape, fan_in, scale=1.0):
        return jax.random.normal(k, (DEPTH,) + shape, jnp.float32) * (scale * fan_in ** -0.5)

    def gain(k, n):
        return 1.0 + 0.01 * jax.random.normal(k, (DEPTH, n), jnp.float32)

    def bias(k, n):
        return 0.01 * jax.random.normal(k, (DEPTH, n), jnp.float32)

    return {
        'x_prompt': jax.random.normal(ks[0], (BATCH, SEQ, D_MODEL), jnp.float32),
        'x_sample': jax.random.normal(ks[1], (DEC_BATCH, DEC_SEQ, D_MODEL), jnp.float32),
        'ffn1_w_in': w(ks[2], (D_MODEL, 2 * D_FF), D_MODEL),
        'ffn1_w_out': w(ks[3], (D_FF, D_MODEL), D_FF, DEEPNORM_BETA),
        'ln1_g': gain(ks[4], D_MODEL),
        'ln1_b': bias(ks[5], D_MODEL),
        'w_in_mix': w(ks[6], (D_MODEL, IN_COLS), D_MODEL),
        'b_gate': bias(ks[7], 2 * D_MODEL),
        'q_norm_g': gain(ks[8], Q_LORA),
        'w_uq': w(ks[9], (Q_LORA, MLA_HEADS * (QK_NOPE + QK_ROPE)), Q_LORA),
        'kv_norm_g': gain(ks[10], KV_LORA),
        'w_ukv': w(ks[11], (KV_LORA, MLA_HEADS * (QK_NOPE + V_DIM)), KV_LORA),
        'w_branch_a': w(ks[12], (MLA_HEADS * V_DIM, D_MODEL), MLA_HEADS * V_DIM),
        'w_branch_b': w(ks[13], (HEADS_PER_GROUP * DIL_HEAD_DIM, D_MODEL), HEADS_PER_GROUP * DIL_HEAD_DIM),
        'w_out_mix': w(ks[14], (D_MODEL, D_MODEL), D_MODEL, DEEPNORM_BETA),
        'ln2_g': gain(ks[15], D_MODEL),
        'ln2_b': bias(ks[16], D_MODEL),
        'ffn2_w_in': w(ks[17], (D_MODEL, 2 * D_FF), D_MODEL),
        'ffn2_w_out': w(ks[18], (D_FF, D_MODEL), D_FF, DEEPNORM_BETA),
        'ln3_g': gain(ks[19], D_MODEL),
        'ln3_b': bias(ks[20], D_MODEL),
    }


def reference(x_prompt, x_sample, ffn1_w_in, ffn1_w_out, ln1_g, ln1_b, w_in_mix, b_gate, q_norm_g, w_uq,
              kv_norm_g, w_ukv, w_branch_a, w_branch_b, w_out_mix, ln2_g, ln2_b, ffn2_w_in, ffn2_w_out,
              ln3_g, ln3_b):
    y_prompt, y_sample = x_prompt, x_sample
    for i in range(DEPTH):
        params = (ffn1_w_in[i], ffn1_w_out[i], ln1_g[i], ln1_b[i], w_in_mix[i], b_gate[i], q_norm_g[i], w_uq[i],
                  kv_norm_g[i], w_ukv[i], w_branch_a[i], w_branch_b[i], w_out_mix[i], ln2_g[i], ln2_b[i],
                  ffn2_w_in[i], ffn2_w_out[i], ln3_g[i], ln3_b[i])
        y_prompt = encoder_layer(y_prompt, *params)
        y_sample = encoder_layer(y_sample, *params)
    return (y_prompt, y_sample)
```

```python
import numpy as np
from contextlib import ExitStack
import concourse.bass as bass
import concourse.mybir as mybir
from concourse.bass_utils import run_bass_kernel_spmd

F32 = mybir.dt.float32
BF16 = mybir.dt.bfloat16
I32 = mybir.dt.int32
AF = mybir.ActivationFunctionType
ALU = mybir.AluOpType

NCORES = 8
D = 2048
FF = 5504
NFT = 43
KC = 16
NH = 8
HPG = 4
ALPHA = float(2.0 ** 0.25)
LN_EPS = 1e-5
RMS_EPS = 1e-6
DILS = (1, 4, 16)
HALF = 64
MLA_SCALE = float(192.0 ** -0.5)
DIL_SCALE = float(128.0 ** -0.5)
OOB = 1 << 30
DEBUG_G = False
SKIP = ""

KT_OFF = 0
KPE_OFF = 1024
KD_OFF = 1088
V_OFF = KD_OFF + 1536
VD_OFF = V_OFF + 1024
RB = VD_OFF + 1536


class Buf:
    __slots__ = ("name", "w", "r", "base")

    def __init__(self, name):
        self.name = name
        self.w = {}
        self.r = {}
        self.base = {}


class Eng:
    def __init__(self, K, name, kind):
        self.K = K
        self.name = name
        self.kind = kind
        self.sem = K.new_sem(name + "_prog")
        self.n = 0
        self.waited = {}
        self.prog = []
        self.events = []
        self.ring = []
        self.ri = 0

    def wait_tok(self, tok):
        sem, val = tok
        if self.kind == "pe" and sem is self.sem:
            return
        k = id(sem)
        if self.waited.get(k, 0) >= val:
            return
        self.waited[k] = val
        self.events.append(("w", k, val))
        self.prog.append(lambda e, sem=sem, val=val: e.wait_ge(sem, val))

    def wait_all(self, d):
        for tok in list(d.values()):
            self.wait_tok(tok)


class Kern:
    def __init__(self, nc, es):
        self.nc = nc
        self.es = es
        self.nsem = 0
        self.pe = Eng(self, "pe", "pe")
        self.act = Eng(self, "act", "act")
        self.dve = Eng(self, "dve", "dve")
        self.pool = Eng(self, "pool", "pool")
        self.sp = Eng(self, "sp", "sp")
        for q, n in ((self.sp, 24), (self.pool, 8)):
            q.ring = [[self.new_sem(f"{q.name}_d{i}"), 0] for i in range(n)]
        self.bg = []
        self.cc_toks = []

    def new_sem(self, name):
        self.nsem += 1
        return self.es.enter_context(self.nc.semaphore(name))

    def _pre(self, E, reads, writes, pwrites):
        for b in reads:
            E.wait_all(b.w)
        for b in writes:
            E.wait_all(b.w)
            E.wait_all(b.r)
        for b in pwrites:
            E.wait_all(b.r)
            E.wait_all(b.base)

    def _post(self, tok, reads, writes, pwrites):
        k = id(tok[0])
        for b in reads:
            b.r[k] = tok
        for b in writes:
            b.w = {k: tok}
            b.base = {k: tok}
            b.r = {}
        for b in pwrites:
            b.w[k] = tok

    def op(self, E, fn, reads=(), writes=(), pwrites=()):
        self._pre(E, reads, writes, pwrites)
        E.n += 1
        tok = (E.sem, E.n)
        E.events.append(("i", id(E.sem), 1))
        E.prog.append(lambda e, fn=fn, sem=E.sem: fn(e).then_inc(sem, 1))
        self._post(tok, reads, writes, pwrites)
        return tok

    def mm(self, out_buf, mms, reads, start=True, stop=True, pw=False, transpose=False):
        E = self.pe
        for b in reads:
            E.wait_all(b.w)
        if pw:
            E.wait_all(out_buf.r)
            E.wait_all(out_buf.base)
        else:
            E.wait_all(out_buf.w)
            E.wait_all(out_buf.r)
        E.n += 1
        tok = (E.sem, E.n)
        E.events.append(("i", id(E.sem), 1))
        n = len(mms)
        for i, (o, l, r) in enumerate(mms):
            last = i == n - 1

            def f(e, o=o, l=l, r=r, i=i, last=last, sem=E.sem):
                if transpose:
                    ins = e.transpose(o, l, r)
                else:
                    ins = e.matmul(o, l, r, start=(start and i == 0), stop=(stop and last))
                if last:
                    ins.then_inc(sem, 1)
            E.prog.append(f)
        k = id(tok[0])
        for b in reads:
            b.r[k] = tok
        if pw:
            out_buf.w[k] = tok
        else:
            out_buf.w = {k: tok}
            out_buf.base = {k: tok}
            out_buf.r = {}
        return tok

    def _ring_tok(self, Q):
        ent = Q.ring[Q.ri]
        Q.ri = (Q.ri + 1) % len(Q.ring)
        if ent[1] > 0:
            Q.wait_tok((ent[0], ent[1]))
        ent[1] += 16
        return ent[0], (ent[0], ent[1])

    def dma(self, Q, out, in_, reads=(), writes=(), pwrites=()):
        self._pre(Q, reads, writes, pwrites)
        sem, tok = self._ring_tok(Q)
        Q.events.append(("i", id(sem), 16))
        Q.prog.append(lambda e, out=out, in_=in_, sem=sem: e.dma_start(out=out, in_=in_).then_inc(sem, 16))
        self._post(tok, reads, writes, pwrites)
        return tok

    def gather(self, out, table, idx_ap, elem_off, bound, reads=(), writes=(), pwrites=()):
        Q = self.pool
        self._pre(Q, reads, writes, pwrites)
        sem, tok = self._ring_tok(Q)
        Q.events.append(("i", id(sem), 16))

        def f(e):
            e.indirect_dma_start(
                out=out, out_offset=None, in_=table,
                in_offset=bass.IndirectOffsetOnAxis(ap=idx_ap, axis=0),
                element_offset=elem_off, bounds_check=None, oob_is_err=False,
            ).then_inc(sem, 16)
        Q.prog.append(f)
        self._post(tok, reads, writes, pwrites)
        return tok

    def allgather(self, in_ap, out_ap, reads, writes):
        Q = self.pool
        self._pre(Q, reads, writes, ())
        sem = self.new_sem(f"cc{self.nsem}")
        tok = (sem, 1)
        self.cc_toks.append(tok)
        Q.events.append(("i", id(sem), 1))

        def f(e):
            e.collective_compute(
                "AllGather", ALU.bypass, replica_groups=[list(range(NCORES))],
                ins=[in_ap.opt()], outs=[out_ap.opt()],
            ).then_inc(sem)
        Q.prog.append(f)
        self._post(tok, reads, writes, ())
        Q.wait_tok(tok)
        return tok

    def all_tokens(self):
        toks = list(self.cc_toks)
        for E in (self.pe, self.act, self.dve, self.pool, self.sp):
            if E.n > 0:
                toks.append((E.sem, E.n))
            for ent in E.ring:
                if ent[1] > 0:
                    toks.append((ent[0], ent[1]))
        return toks

    def barrier(self):
        toks = self.all_tokens()
        for E in (self.pe, self.act, self.dve, self.pool, self.sp):
            for t in toks:
                if t[0] is E.sem:
                    continue
                E.wait_tok(t)

    def check_deadlock(self):
        engs = [self.pe, self.act, self.dve, self.pool, self.sp]
        pc = [0] * 5
        sems = {}
        while True:
            progress = False
            for i, E in enumerate(engs):
                while pc[i] < len(E.events):
                    kind, k, v = E.events[pc[i]]
                    if kind == "w":
                        if sems.get(k, 0) < v:
                            break
                    else:
                        sems[k] = sems.get(k, 0) + v
                    pc[i] += 1
                    progress = True
            if all(pc[i] == len(E.events) for i, E in enumerate(engs)):
                return True
            if not progress:
                names = {}
                for E in engs:
                    names[id(E.sem)] = E.name + "_prog"
                    for j, ent in enumerate(E.ring):
                        names[id(ent[0])] = f"{E.name}_ring{j}"
                for i, E in enumerate(engs):
                    if pc[i] < len(E.events):
                        kind, k, v = E.events[pc[i]]
                        print(f"DEADLOCK {E.name}: event {pc[i]}/{len(E.events)} waits {names.get(k, k)} >= {v} (now {sems.get(k, 0)})")
                    else:
                        print(f"DEADLOCK {E.name}: finished")
                return False

    def bg_step(self, n=1):
        for _ in range(n):
            if self.bg:
                self.bg.pop(0)()

    def bg_drain(self):
        while self.bg:
            self.bg.pop(0)()


class Arena:
    def __init__(self, t, n):
        self.t = t
        self.n = n
        self.off = 0

    def reset(self):
        self.off = 0

    def bf(self, n):
        n2 = (n + 1) // 2 * 2
        assert self.off + n2 <= self.n, (self.off, n2, self.n)
        ap = self.t[:, self.off:self.off + n]
        self.off += n2
        return ap

    def f32(self, n):
        assert self.off + 2 * n <= self.n, (self.off, n, self.n)
        ap = self.t[:, self.off:self.off + 2 * n].bitcast(F32)
        self.off += 2 * n
        return ap

    def i32(self, n):
        assert self.off + 2 * n <= self.n
        ap = self.t[:, self.off:self.off + 2 * n].bitcast(I32)
        self.off += 2 * n
        return ap


class WStream:
    def __init__(self, K, seq, src_buf, slots, bg=True):
        self.K = K
        self.seq = seq
        self.src_buf = src_buf
        self.slots = slots
        self.issued = 0
        self.cur = 0
        self.bgflag = bg

    def next(self):
        K = self.K
        ns = len(self.slots)
        while self.issued < min(len(self.seq), self.cur + ns):
            b, ap = self.slots[self.issued % ns]
            K.dma(K.sp, ap, self.seq[self.issued], reads=[self.src_buf], writes=[b])
            self.issued += 1
        if self.bgflag:
            K.bg_step(1)
        b, ap = self.slots[self.cur % ns]
        self.cur += 1
        return b, ap


class SeqInfo:
    def __init__(self, name, S, row0):
        self.name = name
        self.S = S
        self.c = S // NCORES
        self.row0 = row0
        self.TB = min(512, self.c)
        self.nb = -(-1024 // self.c)


def dil_units(c):
    units = []
    for g, dil in enumerate(DILS):
        Ls = c // dil
        Lq = min(128, Ls)
        for r in range(dil):
            for l0 in range(0, Ls, Lq):
                units.append((g, r, l0, Lq))
    return units


def build_program(seq_lens, upto=None, skip=""):
    nc = bass.Bass("TRN2", target_bir_lowering=False)
    es = ExitStack()
    K = Kern(nc, es)
    seqs = []
    row = 0
    for i, S in enumerate(seq_lens):
        si = SeqInfo(f"q{i}", S, row)
        seqs.append(si)
        row += si.c
    T = row
    cmax = max(s.c for s in seqs)
    ntile_v = sum(2 * len(dil_units(s.c)) for s in seqs)
    nkidx = sum(2 * s.nb for s in seqs)

    def din(name, shape, dt=F32):
        return nc.dram_tensor(name, list(shape), dt, kind="ExternalInput").ap()

    def dscr(name, shape, dt=BF16):
        return nc.dram_tensor(name, list(shape), dt).ap()

    xs = din("xs", [T, D])
    y_out = nc.dram_tensor("y", [T, D], F32, kind="ExternalOutput").ap()
    w_ffn = [(din(f"ffn{i}_w_in", [D, 2 * FF]), din(f"ffn{i}_w_out", [FF, D])) for i in (1, 2)]
    w_in_mix = din("w_in_mix", [D, 9792])
    w_uq = din("w_uq", [512, 1536])
    w_ukv = din("w_ukv", [512, 2048])
    w_a = din("w_branch_a", [1024, D])
    w_b = din("w_branch_b", [512, D])
    w_om = din("w_out_mix", [D, D])
    lnp = din("lnp", [6, 128, D])
    vecs = din("vecs", [128, 40])
    cos128 = din("cos128", [128, T])
    sin128 = din("sin128", [128, T])
    cos64 = din("cos64", [64, T])
    sin64 = din("sin64", [64, T])
    cmat = din("cmat", [128, 6 * 128], BF16)
    cmatf = din("cmatf", [128, 2 * 128])
    idxv_d = din("idxv", [128, ntile_v], I32)
    validv_d = din("validv", [128, ntile_v])
    idxk_d = din("idxk", [128, nkidx], I32)

    def fm_t(name, nft, kc, w=128):
        return dscr(name, [nft, 128, kc, w])

    wi = [fm_t(f"wi{i}", 86, KC) for i in (1, 2)]
    wo = [fm_t(f"wo{i}", 16, NFT) for i in (1, 2)]
    wq_c = fm_t("wq_c", 4, KC)
    wkv_c = fm_t("wkv_c", 4, KC)
    wkr = fm_t("wkr", 1, KC, 64)
    wqd = fm_t("wqd", 12, KC)
    wkd = fm_t("wkd", 12, KC)
    wg = fm_t("wg", 32, KC)
    wvd = dscr("wvd", [D, 1536])
    wuq_n = fm_t("wuq_n", 8, 4)
    wuq_p = fm_t("wuq_p", 8, 4, 64)
    wuk = fm_t("wuk", 8, 4)
    wuv = dscr("wuv", [512, 1024])
    wa_f = fm_t("wa_f", 16, 8)
    wb_f = fm_t("wb_f", 16, 4)
    wom_f = fm_t("wom_f", 16, KC)
    x1_d = dscr("x1", [T, D], F32)
    x2_d = dscr("x2", [T, D], F32)
    blobs = [dscr(f"blob{i}", [RB, s.c]) for i, s in enumerate(seqs)]
    gaths = [dscr(f"gath{i}", [NCORES * RB, s.c]) for i, s in enumerate(seqs)]
    qn_d = dscr("qn", [8, 128, cmax])
    qp_d = dscr("qp", [8, 64, cmax])
    qd_d = dscr("qd", [12, 128, cmax])
    oa_d = dscr("oa", [8, 128, cmax])
    ob_d = dscr("ob", [4, 128, cmax])

    B = {}

    def buf(name):
        if name not in B:
            B[name] = Buf(name)
        return B[name]

    consts = es.enter_context(nc.sbuf_tensor("consts", [128, 6 * 128], BF16))
    constf = es.enter_context(nc.sbuf_tensor("constf", [128, 2 * 128], F32))
    vecs_sb = es.enter_context(nc.sbuf_tensor("vecs_sb", [128, 40], F32))
    epsc = es.enter_context(nc.sbuf_tensor("epsc", [128, 2], F32))
    NA = 95 * 1024
    arena_t = es.enter_context(nc.sbuf_tensor("arena", [128, NA], BF16))
    AR = Arena(arena_t, NA)
    ps = [es.enter_context(nc.psum_tensor(f"ps{i}", [128, 512], F32)) for i in range(8)]
    psb = [buf(f"ps{i}") for i in range(8)]
    ident = consts[:, 0:128]
    ones_bf = consts[:, 128:256]
    r128T = consts[:, 256:384]
    r64T = consts[:, 384:512]
    maskA = consts[:, 512:640]
    maskB = consts[:, 640:768]
    ident_f = constf[:, 0:128]
    ones_f = constf[:, 128:256]
    cb = buf("consts")
    K.dma(K.sp, consts[:, :], cmat, writes=[cb])
    K.dma(K.sp, constf[:, :], cmatf, pwrites=[cb])
    K.dma(K.sp, vecs_sb[:, :], vecs, pwrites=[cb])
    K.op(K.pool, lambda e: e.memset(epsc[:, 0:1], LN_EPS), pwrites=[cb])
    K.op(K.pool, lambda e: e.memset(epsc[:, 1:2], RMS_EPS), pwrites=[cb])

    AR.reset()
    NST = 2
    st32 = [(buf(f"c32_{i}"), AR.f32(KC * 128)) for i in range(NST)]
    st16 = [(buf(f"c16_{i}"), AR.bf(KC * 128)) for i in range(NST)]
    cst = {"i": 0}
    wbuf = buf("wscratch")

    def conv_piece(src_view, dst_view, n, shape3, eng):
        i = cst["i"] % NST
        cst["i"] += 1
        b32, a32 = st32[i]
        b16, a16 = st16[i]
        v32 = a32[:, 0:n]
        v16 = a16[:, 0:n]
        if shape3 is not None:
            v32 = v32.rearrange("p (k c) -> p k c", k=shape3)
            v16 = v16.rearrange("p (k c) -> p k c", k=shape3)
        K.dma(K.sp, v32, src_view, writes=[b32])
        E = (K.pool, K.act, K.dve)[eng]
        if eng == 1:
            K.op(E, lambda e: e.activation(a16[:, 0:n], a32[:, 0:n], AF.Copy), reads=[b32], writes=[b16])
        else:
            K.op(E, lambda e: e.tensor_copy(a16[:, 0:n], a32[:, 0:n]), reads=[b32], writes=[b16])
        K.dma(K.sp, dst_view, v16, reads=[b16], pwrites=[wbuf])

    def conv_fm(dst, src, col0, nft, kc, w=128, colstep=None, engs=(0,)):
        jobs = []
        cnt = 0
        for ft in range(nft):
            c0 = col0 + ft * (colstep if colstep is not None else w)
            for k0 in range(0, kc, KC):
                nk = min(KC, kc - k0)

                def job(ft=ft, c0=c0, k0=k0, nk=nk, eng=engs[cnt % len(engs)]):
                    conv_piece(src[k0 * 128:(k0 + nk) * 128, c0:c0 + w].rearrange("(k p) c -> p k c", p=128),
                               dst[ft][:, k0:k0 + nk, :], nk * w, nk, eng)
                jobs.append(job)
                cnt += 1
        return jobs

    def conv_tm(dst, src, col0, ncols, dcol0=0, engs=(0,)):
        jobs = []
        nr = src.shape[0]
        for r0 in range(0, nr, 128):
            def job(r0=r0):
                conv_piece(src[r0:r0 + 128, col0:col0 + ncols], dst[r0:r0 + 128, dcol0:dcol0 + ncols], ncols, None, engs[0])
            jobs.append(job)
        return jobs

    for j in conv_fm(wi[0], w_ffn[0][0], 0, 86, KC, engs=(0, 1, 2)) + conv_fm(wo[0], w_ffn[0][1], 0, 16, NFT, engs=(0, 1, 2)):
        j()
    K.bg += conv_fm(wq_c, w_in_mix, 0, 4, KC) + conv_fm(wkv_c, w_in_mix, 512, 4, KC)
    K.bg += conv_fm(wkr, w_in_mix, 1024, 1, KC, 64)
    K.bg += conv_fm(wqd, w_in_mix, 1088, 12, KC) + conv_fm(wkd, w_in_mix, 2624, 12, KC)
    K.bg += conv_tm(wvd, w_in_mix, 4160, 1536)
    K.bg += conv_fm(wg, w_in_mix, 5696, 32, KC)
    K.bg += conv_fm(wuq_n, w_uq, 0, 8, 4, 128, 192) + conv_fm(wuq_p, w_uq, 128, 8, 4, 64, 192)
    K.bg += conv_fm(wuk, w_ukv, 0, 8, 4, 128, 256)
    for h in range(8):
        K.bg += conv_tm(wuv, w_ukv, h * 256 + 128, 128, h * 128)
    K.bg += conv_fm(wa_f, w_a, 0, 16, 8) + conv_fm(wb_f, w_b, 0, 16, 4) + conv_fm(wom_f, w_om, 0, 16, KC)
    K.bg += conv_fm(wi[1], w_ffn[1][0], 0, 86, KC) + conv_fm(wo[1], w_ffn[1][1], 0, 16, NFT)
    ARENA_BASE = AR.off

    def stage_begin():
        K.barrier()
        AR.off = ARENA_BASE

    def slots(prefix, n, size, view=None):
        out = []
        for i in range(n):
            ap = AR.bf(size)
            if view is not None:
                ap = view(ap)
            out.append((buf(f"{prefix}{i}"), ap))
        return out

    def load_xT(src, src_buf, r0, TB, xf, xfb, xT, xTb, xbs):
        nt = TB // 128
        for i in range(nt):
            K.dma(K.pool, xf[:, i, :], src[r0 + i * 128:r0 + (i + 1) * 128, :], reads=[src_buf], writes=[xfb[i]])
        for i in range(nt):
            xbb, xb = xbs[i % len(xbs)]
            K.op(K.act, lambda e, xb=xb, i=i: e.activation(xb, xf[:, i, :], AF.Copy), reads=[xfb[i]], writes=[xbb])
            for k4 in range(4):
                pb = psb[6 + (k4 % 2)]
                pt = ps[6 + (k4 % 2)][:, :].bitcast(BF16)
                K.mm(pb, [(pt[:, j * 128:(j + 1) * 128], xb[:, (4 * k4 + j) * 128:(4 * k4 + j + 1) * 128], ident)
                          for j in range(4)], reads=[xbb, cb], transpose=True)
                K.op(K.dve, lambda e, pt=pt, k4=k4, i=i: e.tensor_copy(
                    xT[:, 4 * k4:4 * k4 + 4, i * 128:(i + 1) * 128],
                    pt[:, 0:512].rearrange("p (a b) -> p a b", a=4)), reads=[pb], pwrites=[xTb])

    def layernorm_store(xf, xfb, nt, gam, bet, lb, dst, dst_buf, r0, st, mv, sb_):
        for i in range(nt):
            xi = xf[:, i, :]
            for q in range(4):
                K.op(K.dve, lambda e, i=i, q=q: e.bn_stats(st[:, q * 6:(q + 1) * 6], xf[:, i, q * 512:(q + 1) * 512]),
                     reads=[xfb[i]], pwrites=[sb_])
            K.op(K.dve, lambda e: e.bn_aggr(mv[:, 0:2], st[:, 0:24]), reads=[sb_], writes=[sb_])
            K.op(K.act, lambda e: e.activation(mv[:, 2:3], mv[:, 1:2], AF.Sqrt, bias=epsc[:, 0:1]), reads=[sb_, cb], writes=[sb_])
            K.op(K.dve, lambda e: e.reciprocal(mv[:, 2:3], mv[:, 2:3]), reads=[sb_], writes=[sb_])
            K.op(K.dve, lambda e, xi=xi: e.tensor_scalar(xi, xi, mv[:, 0:1], mv[:, 2:3], ALU.subtract, ALU.mult),
                 reads=[sb_], writes=[xfb[i]])
            K.op(K.pool, lambda e, xi=xi: e.tensor_tensor(xi, xi, gam, ALU.mult), reads=[lb], writes=[xfb[i]])
            K.op(K.pool, lambda e, xi=xi: e.tensor_tensor(xi, xi, bet, ALU.add), reads=[lb], writes=[xfb[i]])
            K.dma(K.pool, dst[r0 + i * 128:r0 + (i + 1) * 128, :], xi, reads=[xfb[i]], pwrites=[dst_buf])

    def out_proj_ln(hT, hTb, nkc, wstream, coef, TB, xf, xfb, ystage, gam, bet, lb, dst, dst_buf, r0, st, mv, sb_):
        nt = TB // 128

        def transposes(dt):
            ysb, ys = ystage[dt % 2]
            for i in range(nt):
                K.mm(psb[i], [(ps[i][:, (dt % 4) * 128:(dt % 4 + 1) * 128], ys[:, i * 128:(i + 1) * 128], ident_f)],
                     reads=[ysb, cb], transpose=True, pw=(dt % 4 != 0))
            if dt % 4 == 3:
                g4 = dt // 4
                for i in range(nt):
                    K.op(K.dve, lambda e, i=i, g4=g4: e.scalar_tensor_tensor(
                        xf[:, i, g4 * 512:(g4 + 1) * 512], xf[:, i, g4 * 512:(g4 + 1) * 512], ALPHA, ps[i][:, :],
                        ALU.mult, ALU.add), reads=[psb[i]], writes=[xfb[i]])

        for dt in range(16):
            wb_, wt = wstream.next()
            ab = psb[4 + dt % 2]
            acc = ps[4 + dt % 2]
            K.mm(ab, [(acc[:, :TB], wt[:, k, :], hT[:, k, :]) for k in range(nkc)], reads=[wb_, hTb])
            ysb, ys = ystage[dt % 2]
            K.op(K.act, lambda e, ys=ys, acc=acc: e.activation(ys[:, :TB], acc[:, :TB], AF.Copy, scale=coef),
                 reads=[ab], writes=[ysb])
            if dt >= 1:
                transposes(dt - 1)
        transposes(15)
        layernorm_store(xf, xfb, nt, gam, bet, lb, dst, dst_buf, r0, st, mv, sb_)

    def rope_fm(src_ps, srcb, nrow, TB, cosT, sinT, rT, out_bf, outb, rp, rot_i):
        tmpx, tmpxb, tmpb, tmpbb, tmp2, tmp2b, tbl_b = rp
        rotb, rot_ps = psb[rot_i], ps[rot_i]
        K.op(K.act, lambda e: e.activation(tmpx[0:nrow, :TB], src_ps[0:nrow, :TB], AF.Copy), reads=[srcb], writes=[tmpxb])
        K.op(K.dve, lambda e: e.tensor_copy(tmpb[0:nrow, :TB], tmpx[0:nrow, :TB]), reads=[tmpxb], writes=[tmpbb])
        K.mm(rotb, [(rot_ps[0:nrow, :TB], rT[0:nrow, 0:nrow], tmpb[0:nrow, :TB])], reads=[tmpbb, cb])
        K.op(K.dve, lambda e: e.tensor_tensor(tmpx[0:nrow, :TB], tmpx[0:nrow, :TB], cosT, ALU.mult),
             reads=[tbl_b], writes=[tmpxb])
        K.op(K.dve, lambda e: e.tensor_tensor(tmp2[0:nrow, :TB], sinT, rot_ps[0:nrow, :TB], ALU.mult),
             reads=[tbl_b, rotb], writes=[tmp2b])
        K.op(K.dve, lambda e: e.tensor_tensor(out_bf, tmpx[0:nrow, :TB], tmp2[0:nrow, :TB], ALU.add),
             reads=[tmpxb, tmp2b], writes=[outb])

    def ffn_stage(src, src_buf, dst, dst_buf, row0, ntok, TB, wi_d, wo_d, ln_idx):
        stage_begin()
        nt = TB // 128
        nblk = ntok // TB
        xf = AR.f32(nt * D).rearrange("p (t d) -> p t d", t=nt)
        xfb = [buf(f"xf{i}") for i in range(nt)]
        gam = AR.f32(D)
        bet = AR.f32(D)
        lb = buf("lnp")
        K.dma(K.pool, gam, lnp[ln_idx], writes=[lb])
        K.dma(K.pool, bet, lnp[ln_idx + 1], pwrites=[lb])
        st = AR.f32(24)
        mv = AR.f32(4)
        sb_ = buf("lnstat")
        ystage = [(buf(f"ys{i}"), AR.f32(512)) for i in range(2)]
        sg = [(buf(f"sg{i}"), AR.f32(512)) for i in range(2)]
        xT = AR.bf(KC * TB).rearrange("p (k t) -> p k t", k=KC)
        xTb = buf("xT")
        hT = AR.bf(NFT * TB).rearrange("p (k t) -> p k t", k=NFT)
        hTb = buf("hT")
        xbs = slots("xb", 2, D)
        wis = slots("wis", 4, KC * 128, lambda a: a.rearrange("p (k c) -> p k c", k=KC))
        wos = slots("wos", 2, NFT * 128, lambda a: a.rearrange("p (k c) -> p k c", k=NFT))
        seq_i = []
        for _ in range(nblk):
            for j in range(NFT):
                seq_i += [wi_d[j], wi_d[NFT + j]]
        wi_s = WStream(K, seq_i, wbuf, wis)
        wo_s = WStream(K, [wo_d[dt] for _ in range(nblk) for dt in range(16)], wbuf, wos)
        for b in range(nblk):
            r0 = row0 + b * TB
            load_xT(src, src_buf, r0, TB, xf, xfb, xT, xTb, xbs)
            for j in range(NFT):
                pg, pu = (0, 1) if j % 2 == 0 else (2, 3)
                gb_, gw = wi_s.next()
                K.mm(psb[pg], [(ps[pg][:, :TB], gw[:, k, :], xT[:, k, :]) for k in range(KC)], reads=[gb_, xTb])
                ub_, uw = wi_s.next()
                K.mm(psb[pu], [(ps[pu][:, :TB], uw[:, k, :], xT[:, k, :]) for k in range(KC)], reads=[ub_, xTb])
                sgb, sga = sg[j % 2]
                K.op(K.act, lambda e, sga=sga, pg=pg: e.activation(sga[:, :TB], ps[pg][:, :TB], AF.Silu),
                     reads=[psb[pg]], writes=[sgb])
                K.op(K.dve, lambda e, sga=sga, pu=pu, j=j: e.tensor_tensor(hT[:, j, :], sga[:, :TB], ps[pu][:, :TB], ALU.mult),
                     reads=[sgb, psb[pu]], pwrites=[hTb])
            out_proj_ln(hT, hTb, NFT, wo_s, 0.5, TB, xf, xfb, ystage, gam, bet, lb, dst, dst_buf, r0, st, mv, sb_)
            hTb.w = dict(hTb.w)

    def m1_stage(si, sidx):
        stage_begin()
        K.bg_drain()
        c, TB = si.c, si.TB
        nt = TB // 128
        blob = blobs[sidx]
        bb = buf(f"blob{sidx}")
        blob_v = blob[V_OFF:V_OFF + 1024, :].rearrange("(h p) c -> p h c", p=128)
        blob_flat = blob.rearrange("a c -> (a c)").rearrange("(n w) -> n w", w=512)
        xf = AR.f32(nt * D).rearrange("p (t d) -> p t d", t=nt)
        xfb = [buf(f"xf{i}") for i in range(nt)]
        xT = AR.bf(KC * TB).rearrange("p (k t) -> p k t", k=KC)
        xTb = buf("xT")
        xbs = slots("xb", 2, D)
        ckv = AR.f32(4 * TB).rearrange("p (k t) -> p k t", k=4)
        sq = AR.bf(4 * TB).rearrange("p (k t) -> p k t", k=4)
        ckvb, sqb = buf("ckv"), buf("sq")
        rstd = AR.f32(TB)
        rstdb = buf("rstd")
        kvn = AR.bf(4 * TB).rearrange("p (k t) -> p k t", k=4)
        kvnb = buf("kvn")
        tmpx = AR.f32(TB)
        tmpb = AR.bf(TB)
        tmpxb, tmpbb = buf("tmpx"), buf("tmpb")
        tmp2 = AR.f32(TB)
        tmp2b = buf("tmp2")
        cs128 = AR.f32(TB)
        sn128 = AR.f32(TB)
        cs64 = AR.f32(TB)
        sn64 = AR.f32(TB)
        tblb = buf("ropetbl")
        rp = (tmpx, tmpxb, tmpb, tmpbb, tmp2, tmp2b, tblb)
        outs = slots("m1o", 3, 1024)
        wfm = slots("wfm", 4, KC * 128, lambda a: a.rearrange("p (k c) -> p k c", k=KC))
        wkr_sb = AR.bf(KC * 64).rearrange("p (k c) -> p k c", k=KC)
        wuk_sb = AR.bf(8 * 4 * 128).rearrange("p (h k c) -> p h k c", h=8, k=4)
        wuv_sb = AR.bf(4 * 1024).rearrange("p (k c) -> p k c", k=4)
        wvd_s = slots("wvd", 2, KC * 512, lambda a: a.rearrange("p (k c) -> p k c", k=KC))
        wsm = buf("m1wsmall")
        K.dma(K.sp, wkr_sb, wkr[0], reads=[wbuf], writes=[wsm])
        K.dma(K.sp, wuk_sb, wuk.rearrange("h p k c -> p h k c"), reads=[wbuf], pwrites=[wsm])
        K.dma(K.sp, wuv_sb, wuv.rearrange("(k p) c -> p k c", p=128), reads=[wbuf], pwrites=[wsm])
        nblk = c // TB
        seqw = []
        for _ in range(nblk):
            seqw += [wkv_c[f] for f in range(4)] + [wkd[f] for f in range(12)]
        ws = WStream(K, seqw, wbuf, wfm, bg=False)
        wvds = WStream(K, [wvd[:, g * 512:(g + 1) * 512].rearrange("(k p) c -> p k c", p=128)
                           for _ in range(nblk) for g in range(3)], wbuf, wvd_s, bg=False)
        oi = {"i": 0}

        def oslot():
            o = outs[oi["i"] % 3]
            oi["i"] += 1
            return o

        for b in range(nblk):
            t0 = b * TB
            r0 = si.row0 + t0
            load_xT(x1_d, buf("x1"), r0, TB, xf, xfb, xT, xTb, xbs)
            K.dma(K.pool, cs128[:, :TB], cos128[:, r0:r0 + TB], writes=[tblb])
            K.dma(K.pool, sn128[:, :TB], sin128[:, r0:r0 + TB], pwrites=[tblb])
            K.dma(K.pool, cs64[0:64, :TB], cos64[:, r0:r0 + TB], pwrites=[tblb])
            K.dma(K.pool, sn64[0:64, :TB], sin64[:, r0:r0 + TB], pwrites=[tblb])
            for f in range(4):
                wb_, wt = ws.next()
                pb = f % 2
                K.mm(psb[pb], [(ps[pb][:, :TB], wt[:, k, :], xT[:, k, :]) for k in range(KC)], reads=[wb_, xTb])
                K.op(K.act, lambda e, f=f, pb=pb: e.activation(ckv[:, f, :], ps[pb][:, :TB], AF.Copy),
                     reads=[psb[pb]], pwrites=[ckvb])
                K.op(K.act, lambda e, f=f, pb=pb: e.activation(sq[:, f, :], ps[pb][:, :TB], AF.Square),
                     reads=[psb[pb]], pwrites=[sqb])
            K.mm(psb[2], [(ps[2][:, :TB], ones_bf, sq[:, f, :]) for f in range(4)], reads=[sqb, cb])
            K.op(K.act, lambda e: e.activation(rstd[:, :TB], ps[2][:, :TB], AF.Sqrt, bias=epsc[:, 1:2], scale=1.0 / 512),
                 reads=[psb[2], cb], writes=[rstdb])
            K.op(K.dve, lambda e: e.reciprocal(rstd[:, :TB], rstd[:, :TB]), reads=[rstdb], writes=[rstdb])
            for f in range(4):
                K.op(K.dve, lambda e, f=f: e.scalar_tensor_tensor(kvn[:, f, :], ckv[:, f, :], vecs_sb[:, 4 + f:5 + f],
                                                                  rstd[:, :TB], ALU.mult, ALU.mult),
                     reads=[ckvb, rstdb, cb], pwrites=[kvnb])
            kvnb.w = dict(kvnb.w)
            for h in range(8 if "k" not in skip else 0):
                pb = h % 2
                K.mm(psb[pb], [(ps[pb][:, :TB], wuk_sb[:, h, k, :], kvn[:, k, :]) for k in range(4)], reads=[wsm, kvnb])
                ob, oap = oslot()
                K.op(K.act, lambda e, oap=oap, pb=pb: e.activation(oap[:, :TB], ps[pb][:, :TB], AF.Copy),
                     reads=[psb[pb]], writes=[ob])
                K.dma(K.pool, blob[KT_OFF + h * 128:KT_OFF + (h + 1) * 128, t0:t0 + TB], oap[:, :TB], reads=[ob], pwrites=[bb])
            for i in range(nt if "v" not in skip else 0):
                ob, oap = oslot()
                for hf in range(2):
                    pb = 2 + hf
                    K.mm(psb[pb], [(ps[pb][:, :512], kvn[:, k, i * 128:(i + 1) * 128], wuv_sb[:, k, hf * 512:(hf + 1) * 512])
                                   for k in range(4)], reads=[wsm, kvnb])
                    K.op(K.act if hf == 0 else K.dve,
                         (lambda e, oap=oap, pb=pb, hf=hf: e.activation(oap[:, hf * 512:(hf + 1) * 512], ps[pb][:, :512], AF.Copy))
                         if hf == 0 else
                         (lambda e, oap=oap, pb=pb, hf=hf: e.tensor_copy(oap[:, hf * 512:(hf + 1) * 512], ps[pb][:, :512])),
                         reads=[psb[pb]], pwrites=[ob])
                lt = t0 // 128 + i
                K.dma(K.pool, blob_v[:, :, lt * 128:(lt + 1) * 128], oap[:, 0:1024].rearrange("p (h c) -> p h c", h=8),
                      reads=[ob], pwrites=[bb])
                ob.w = dict(ob.w)
            if "p" not in skip:
              K.mm(psb[4], [(ps[4][0:64, :TB], wkr_sb[:, k, :], xT[:, k, :]) for k in range(KC)], reads=[wsm, xTb])
              ob, oap = oslot()
              rope_fm(ps[4], psb[4], 64, TB, cs64[0:64, :TB], sn64[0:64, :TB], r64T, oap[0:64, :TB], ob, rp, 5)
              K.dma(K.pool, blob[KPE_OFF:KPE_OFF + 64, t0:t0 + TB], oap[0:64, :TB], reads=[ob], pwrites=[bb])
            for gh in range(12):
                wb_, wt = ws.next()
                pb = gh % 2
                K.mm(psb[pb], [(ps[pb][:, :TB], wt[:, k, :], xT[:, k, :]) for k in range(KC)], reads=[wb_, xTb])
                ob, oap = oslot()
                if "d" in skip:
                    K.op(K.act, lambda e, oap=oap, pb=pb: e.activation(oap[:, :TB], ps[pb][:, :TB], AF.Copy), reads=[psb[pb]], writes=[ob])
                else:
                    rope_fm(ps[pb], psb[pb], 128, TB, cs128[:, :TB], sn128[:, :TB], r128T, oap[:, :TB], ob, rp, 2 + gh % 2)
                K.dma(K.pool, blob[KD_OFF + gh * 128:KD_OFF + (gh + 1) * 128, t0:t0 + TB], oap[:, :TB], reads=[ob], pwrites=[bb])
            for g in range(3 if "w" not in skip else 0):
                wb_, wt = wvds.next()
                for i in range(nt):
                    pb = 4 + (i % 2)
                    K.mm(psb[pb], [(ps[pb][:, :512], xT[:, k, i * 128:(i + 1) * 128], wt[:, k, :]) for k in range(KC)],
                         reads=[wb_, xTb])
                    ob, oap = oslot()
                    K.op(K.act, lambda e, oap=oap, pb=pb: e.activation(oap[:, :512], ps[pb][:, :512], AF.Copy),
                         reads=[psb[pb]], writes=[ob])
                    row = VD_OFF * c // 512 + g * c + t0 + i * 128
                    K.dma(K.pool, blob_flat[row:row + 128, :], oap[:, :512], reads=[ob], pwrites=[bb])
            xTb.w = dict(xTb.w)
        K.allgather(blob, gaths[sidx], reads=[bb], writes=[buf(f"gath{sidx}")])

    def m2_stage(si, sidx, vt0, kx0):
        c, TB, S = si.c, si.TB, si.S
        nt = TB // 128
        nblk = c // TB
        gath = gaths[sidx]
        gb_ = buf(f"gath{sidx}")
        blob = blobs[sidx]
        bb = buf(f"blob{sidx}")
        qnb, qpb, qdb, oab, obb = buf("qn_d"), buf("qp_d"), buf("qd_d"), buf("oa_d"), buf("ob_d")

        stage_begin()
        xf = AR.f32(nt * D).rearrange("p (t d) -> p t d", t=nt)
        xfb = [buf(f"xf{i}") for i in range(nt)]
        xT = AR.bf(KC * TB).rearrange("p (k t) -> p k t", k=KC)
        xTb = buf("xT")
        xbs = slots("xb", 2, D)
        cq = AR.f32(4 * TB).rearrange("p (k t) -> p k t", k=4)
        sq = AR.bf(4 * TB).rearrange("p (k t) -> p k t", k=4)
        cqb, sqb = buf("ckv"), buf("sq")
        rstd = AR.f32(TB)
        rstdb = buf("rstd")
        qn = AR.bf(4 * TB).rearrange("p (k t) -> p k t", k=4)
        qnsb = buf("kvn")
        tmpx = AR.f32(TB)
        tmpb = AR.bf(TB)
        tmpxb, tmpbb = buf("tmpx"), buf("tmpb")
        tmp2 = AR.f32(TB)
        tmp2b = buf("tmp2")
        cs128 = AR.f32(TB)
        sn128 = AR.f32(TB)
        cs64 = AR.f32(TB)
        sn64 = AR.f32(TB)
        tblb = buf("ropetbl")
        rp = (tmpx, tmpxb, tmpb, tmpbb, tmp2, tmp2b, tblb)
        outs = slots("m1o", 3, 512)
        wfm = slots("wfm", 4, KC * 128, lambda a: a.rearrange("p (k c) -> p k c", k=KC))
        wuqn_sb = AR.bf(8 * 4 * 128).rearrange("p (h k c) -> p h k c", h=8, k=4)
        wuqp_sb = AR.bf(8 * 4 * 64).rearrange("p (h k c) -> p h k c", h=8, k=4)
        wsm = buf("m1wsmall")
        K.dma(K.sp, wuqn_sb, wuq_n.rearrange("h p k c -> p h k c"), reads=[wbuf], writes=[wsm])
        K.dma(K.sp, wuqp_sb, wuq_p.rearrange("h p k c -> p h k c"), reads=[wbuf], pwrites=[wsm])
        seqw = []
        for _ in range(nblk):
            seqw += [wq_c[f] for f in range(4)] + [wqd[f] for f in range(12)]
        ws = WStream(K, seqw, wbuf, wfm, bg=False)
        oi = {"i": 0}

        def oslot():
            o = outs[oi["i"] % 3]
            oi["i"] += 1
            return o

        for b in range(nblk):
            t0 = b * TB
            r0 = si.row0 + t0
            load_xT(x1_d, buf("x1"), r0, TB, xf, xfb, xT, xTb, xbs)
            K.dma(K.pool, cs128[:, :TB], cos128[:, r0:r0 + TB], writes=[tblb])
            K.dma(K.pool, sn128[:, :TB], sin128[:, r0:r0 + TB], pwrites=[tblb])
            K.dma(K.pool, cs64[0:64, :TB], cos64[:, r0:r0 + TB], pwrites=[tblb])
            K.dma(K.pool, sn64[0:64, :TB], sin64[:, r0:r0 + TB], pwrites=[tblb])
            for f in range(4):
                wb_, wt = ws.next()
                pb = f % 2
                K.mm(psb[pb], [(ps[pb][:, :TB], wt[:, k, :], xT[:, k, :]) for k in range(KC)], reads=[wb_, xTb])
                K.op(K.act, lambda e, f=f, pb=pb: e.activation(cq[:, f, :], ps[pb][:, :TB], AF.Copy),
                     reads=[psb[pb]], pwrites=[cqb])
                K.op(K.act, lambda e, f=f, pb=pb: e.activation(sq[:, f, :], ps[pb][:, :TB], AF.Square),
                     reads=[psb[pb]], pwrites=[sqb])
            K.mm(psb[2], [(ps[2][:, :TB], ones_bf, sq[:, f, :]) for f in range(4)], reads=[sqb, cb])
            K.op(K.act, lambda e: e.activation(rstd[:, :TB], ps[2][:, :TB], AF.Sqrt, bias=epsc[:, 1:2], scale=1.0 / 512),
                 reads=[psb[2], cb], writes=[rstdb])
            K.op(K.dve, lambda e: e.reciprocal(rstd[:, :TB], rstd[:, :TB]), reads=[rstdb], writes=[rstdb])
            for f in range(4):
                K.op(K.dve, lambda e, f=f: e.scalar_tensor_tensor(qn[:, f, :], cq[:, f, :], vecs_sb[:, f:f + 1],
                                                                  rstd[:, :TB], ALU.mult, ALU.mult),
                     reads=[cqb, rstdb, cb], pwrites=[qnsb])
            qnsb.w = dict(qnsb.w)
            for h in range(8):
                pb = h % 2
                K.mm(psb[pb], [(ps[pb][:, :TB], wuqn_sb[:, h, k, :], qn[:, k, :]) for k in range(4)], reads=[wsm, qnsb])
                ob, oap = oslot()
                K.op(K.act, lambda e, oap=oap, pb=pb: e.activation(oap[:, :TB], ps[pb][:, :TB], AF.Copy),
                     reads=[psb[pb]], writes=[ob])
                K.dma(K.pool, qn_d[h, :, t0:t0 + TB], oap[:, :TB], reads=[ob], pwrites=[qnb])
                pb2 = 2 + h % 2
                K.mm(psb[pb2], [(ps[pb2][0:64, :TB], wuqp_sb[:, h, k, :], qn[:, k, :]) for k in range(4)], reads=[wsm, qnsb])
                ob, oap = oslot()
                rope_fm(ps[pb2], psb[pb2], 64, TB, cs64[0:64, :TB], sn64[0:64, :TB], r64T, oap[0:64, :TB], ob, rp, 4 + h % 2)
                K.dma(K.pool, qp_d[h, :, t0:t0 + TB], oap[0:64, :TB], reads=[ob], pwrites=[qpb])
            for gh in range(12):
                wb_, wt = ws.next()
                pb = gh % 2
                K.mm(psb[pb], [(ps[pb][:, :TB], wt[:, k, :], xT[:, k, :]) for k in range(KC)], reads=[wb_, xTb])
                ob, oap = oslot()
                rope_fm(ps[pb], psb[pb], 128, TB, cs128[:, :TB], sn128[:, :TB], r128T, oap[:, :TB], ob, rp, 2 + gh % 2)
                K.dma(K.pool, qd_d[gh, :, t0:t0 + TB], oap[:, :TB], reads=[ob], pwrites=[qdb])
            xTb.w = dict(xTb.w)

        stage_begin()
        nkt = S // 128
        NQ = 4 if nkt >= 4 else 1
        kq = nkt // NQ
        TBq = TB
        nqb = c // TBq
        KT = AR.bf(S)
        V = AR.bf(S).rearrange("p (t d) -> p t d", d=128)
        KPE = AR.bf(S)
        ktb = [buf(f"kt{i}") for i in range(NQ)]
        vb = [buf(f"v{i}") for i in range(NQ)]
        kpeb = buf("kpe")
        qns = slots("qns", 2, c)
        qps = slots("qps", 2, c)
        pr = slots("pr", 4, TBq)
        rec = AR.f32(TBq)
        recb = buf("rec")
        ost = slots("ost", 2, TBq)
        gv = gath.rearrange("(r b) c -> b r c", r=NCORES)
        K.dma(K.sp, KPE[0:64, :].rearrange("p (r c) -> p r c", r=NCORES), gv[KPE_OFF:KPE_OFF + 64], reads=[gb_], writes=[kpeb])
        cpr = (kq * 128) // c if (kq * 128) >= c else 0
        for h in range(8):
            for qd_ in range(NQ):
                k0 = qd_ * kq * 128
                if kq * 128 >= c:
                    r_lo = k0 // c
                    nr = kq * 128 // c
                    K.dma(K.sp, KT[:, k0:k0 + kq * 128].rearrange("p (r c) -> p r c", r=nr),
                          gv[KT_OFF + h * 128:KT_OFF + (h + 1) * 128, r_lo:r_lo + nr, :], reads=[gb_], writes=[ktb[qd_]])
                    K.dma(K.sp, V[:, k0 // 128:k0 // 128 + kq, :].rearrange("p (r t) d -> p r (t d)", r=nr),
                          gv[V_OFF + h * 128:V_OFF + (h + 1) * 128, r_lo:r_lo + nr, :], reads=[gb_], writes=[vb[qd_]])
                else:
                    r_lo = k0 // c
                    o_ = k0 % c
                    K.dma(K.sp, KT[:, k0:k0 + kq * 128], gv[KT_OFF + h * 128:KT_OFF + (h + 1) * 128, r_lo, o_:o_ + kq * 128],
                          reads=[gb_], writes=[ktb[qd_]])
                    K.dma(K.sp, V[:, k0 // 128:k0 // 128 + kq, :].rearrange("p t d -> p (t d)"),
                          gv[V_OFF + h * 128:V_OFF + (h + 1) * 128, r_lo, o_:o_ + kq * 128], reads=[gb_], writes=[vb[qd_]])
            qnsb_, qna = qns[h % 2]
            qpsb_, qpa = qps[h % 2]
            K.dma(K.sp, qna[:, :c], qn_d[h, :, 0:c], reads=[qnb], writes=[qnsb_])
            K.dma(K.sp, qpa[0:64, :c], qp_d[h, :, 0:c], reads=[qpb], writes=[qpsb_])
            for qb in range(nqb):
                q0 = qb * TBq
                ob_i, db_i = (4, 5) if (h * nqb + qb) % 2 == 0 else (6, 7)
                pend = []

                def qk(kt):
                    sbk = kt % 3
                    K.mm(psb[sbk], [(ps[sbk][:, :TBq], KT[:, kt * 128:(kt + 1) * 128], qna[:, q0:q0 + TBq]),
                                    (ps[sbk][:, :TBq], KPE[0:64, kt * 128:(kt + 1) * 128], qpa[0:64, q0:q0 + TBq])],
                         reads=[ktb[kt // kq], kpeb, qnsb_, qpsb_])
                    pb_, pa = pr[kt % 4]
                    K.op(K.act, lambda e, pa=pa, sbk=sbk: e.activation(pa[:, :TBq], ps[sbk][:, :TBq], AF.Exp, scale=MLA_SCALE),
                         reads=[psb[sbk]], writes=[pb_])

                def pv(kt):
                    pb_, pa = pr[kt % 4]
                    K.mm(psb[ob_i], [(ps[ob_i][:, :TBq], V[:, kt, :], pa[:, :TBq])], reads=[vb[kt // kq], pb_],
                         start=(kt == 0), stop=(kt == nkt - 1), pw=(kt != 0))
                    K.mm(psb[db_i], [(ps[db_i][:, :TBq], ones_bf, pa[:, :TBq])], reads=[cb, pb_],
                         start=(kt == 0), stop=(kt == nkt - 1), pw=(kt != 0))

                for kt in range(nkt):
                    qk(kt)
                    if kt >= 1:
                        pv(kt - 1)
                pv(nkt - 1)
                K.op(K.dve, lambda e, db_i=db_i: e.reciprocal(rec[:, :TBq], ps[db_i][:, :TBq]), reads=[psb[db_i]], writes=[recb])
                osb, osa = ost[(h * nqb + qb) % 2]
                K.op(K.dve, lambda e, osa=osa, ob_i=ob_i: e.tensor_tensor(osa[:, :TBq], rec[:, :TBq], ps[ob_i][:, :TBq], ALU.mult),
                     reads=[psb[ob_i], recb], writes=[osb])
                K.dma(K.pool, oa_d[h, :, q0:q0 + TBq], osa[:, :TBq], reads=[osb], pwrites=[oab])

        stage_begin()
        nbk = si.nb
        W3 = (2 * nbk + 1) * c
        kwin = slots("kwin", 4, W3)
        qds = slots("qds", 4, c)
        vts = slots("vts", 4, 512)
        ets = slots("ets", 4, 128)
        pts = slots("pts", 4, 128)
        nacc = AR.f32(4 * c).rearrange("p (j t) -> p j t", j=4)
        dacc = AR.f32(4 * c).rearrange("p (j t) -> p j t", j=4)
        naccb = [buf(f"nacc{j}") for j in range(4)]
        daccb = [buf(f"dacc{j}") for j in range(4)]
        idxv = AR.i32(ntile_v)
        validv = AR.f32(ntile_v)
        idxk = AR.i32(nkidx)
        ixb = buf("idx")
        K.dma(K.sp, idxv, idxv_d, writes=[ixb])
        K.dma(K.sp, validv, validv_d, pwrites=[ixb])
        K.dma(K.sp, idxk, idxk_d, pwrites=[ixb])
        gflat = gath.rearrange("a c -> (a c)").rearrange("(n w) -> n w", w=512)
        units = dil_units(c)
        vt = vt0
        ui = 0
        for g, dil in enumerate(DILS):
            for j in range(4):
                gh = g * 4 + j
                kb_, ka = kwin[j]
                K.op(K.pool, lambda e, ka=ka: e.memset(ka[:, 0:W3], 0.0), writes=[kb_])
                K.dma(K.pool, ka[:, nbk * c:(nbk + 1) * c], blob[KD_OFF + gh * 128:KD_OFF + (gh + 1) * 128, :],
                      reads=[bb], pwrites=[kb_])
                for kk in range(2 * nbk):
                    off = kk - nbk if kk < nbk else kk - nbk + 1
                    col = (off + nbk) * c
                    K.gather(ka[:, col:col + c], gath, idxk[:, kx0 + kk:kx0 + kk + 1], (KD_OFF + gh * 128) * c,
                             NCORES * RB - 1, reads=[gb_, ixb], pwrites=[kb_])
                kb_.w = dict(kb_.w)
                qb_, qa = qds[j]
                K.dma(K.sp, qa[:, :c], qd_d[gh, :, 0:c], reads=[qdb], writes=[qb_])
            gunits = [u for u in units if u[0] == g]
            for (_, r, l0, Lq) in gunits:
                vA_b, vA = vts[(2 * ui) % 4]
                vB_b, vB = vts[(2 * ui + 1) % 4]
                eoff = VD_OFF * c + g * c * 512
                K.gather(vA[:, 0:512], gflat, idxv[:, vt:vt + 1], eoff, NCORES * RB * c // 512 - 1, reads=[gb_, ixb], writes=[vA_b])
                K.gather(vB[:, 0:512], gflat, idxv[:, vt + 1:vt + 2], eoff, NCORES * RB * c // 512 - 1, reads=[gb_, ixb], writes=[vB_b])
                for j in range(4):
                    kb_, ka = kwin[j]
                    qb_, qa = qds[j]
                    ca = nbk * c + (l0 - HALF) * dil + r
                    cbb = nbk * c + (l0 + HALF) * dil + r
                    qc = l0 * dil + r
                    qv = qa[:, qc:qc + (Lq - 1) * dil + 1:dil]
                    sA, sB = (0, 1) if j % 2 == 0 else (2, 3)
                    K.mm(psb[sA], [(ps[sA][:, :Lq], ka[:, ca:ca + 127 * dil + 1:dil], qv)], reads=[kb_, qb_])
                    K.mm(psb[sB], [(ps[sB][0:Lq, :Lq], ka[:, cbb:cbb + (Lq - 1) * dil + 1:dil], qv)], reads=[kb_, qb_])
                    eA_b, eA = ets[(2 * j) % 4]
                    eB_b, eB = ets[(2 * j + 1) % 4]
                    K.op(K.act, lambda e, eA=eA, sA=sA, Lq=Lq: e.activation(eA[:, :Lq], ps[sA][:, :Lq], AF.Exp, scale=DIL_SCALE),
                         reads=[psb[sA]], writes=[eA_b])
                    K.op(K.act, lambda e, eB=eB, sB=sB, Lq=Lq: e.activation(eB[0:Lq, :Lq], ps[sB][0:Lq, :Lq], AF.Exp, scale=DIL_SCALE),
                         reads=[psb[sB]], writes=[eB_b])
                    pA_b, pA = pts[(2 * j) % 4]
                    pB_b, pB = pts[(2 * j + 1) % 4]
                    K.op(K.dve, lambda e, pA=pA, eA=eA, vt=vt, Lq=Lq: e.scalar_tensor_tensor(
                        pA[:, :Lq], eA[:, :Lq], validv[:, vt:vt + 1], maskA[:, :Lq], ALU.mult, ALU.mult),
                        reads=[eA_b, ixb, cb], writes=[pA_b])
                    K.op(K.dve, lambda e, pB=pB, eB=eB, vt=vt, Lq=Lq: e.scalar_tensor_tensor(
                        pB[0:Lq, :Lq], eB[0:Lq, :Lq], validv[0:Lq, vt + 1:vt + 2], maskB[0:Lq, :Lq], ALU.mult, ALU.mult),
                        reads=[eB_b, ixb, cb], writes=[pB_b])
                    nb_i, db_i = (4, 5) if j % 2 == 0 else (6, 7)
                    K.mm(psb[nb_i], [(ps[nb_i][:, :Lq], vA[:, j * 128:(j + 1) * 128], pA[:, :Lq]),
                                     (ps[nb_i][:, :Lq], vB[0:Lq, j * 128:(j + 1) * 128], pB[0:Lq, :Lq])],
                         reads=[vA_b, vB_b, pA_b, pB_b])
                    K.mm(psb[db_i], [(ps[db_i][:, :Lq], ones_bf, pA[:, :Lq]),
                                     (ps[db_i][:, :Lq], ones_bf[0:Lq, :], pB[0:Lq, :Lq])],
                         reads=[cb, pA_b, pB_b])
                    nv = nacc[:, j, qc:qc + (Lq - 1) * dil + 1:dil]
                    dv = dacc[:, j, qc:qc + (Lq - 1) * dil + 1:dil]
                    if g == 0:
                        K.op(K.dve, lambda e, nv=nv, nb_i=nb_i, Lq=Lq: e.tensor_copy(nv, ps[nb_i][:, :Lq]),
                             reads=[psb[nb_i]], pwrites=[naccb[j]])
                        K.op(K.act, lambda e, dv=dv, db_i=db_i, Lq=Lq: e.activation(dv, ps[db_i][:, :Lq], AF.Copy),
                             reads=[psb[db_i]], pwrites=[daccb[j]])
                    else:
                        K.op(K.dve, lambda e, nv=nv, nb_i=nb_i, Lq=Lq: e.tensor_tensor(nv, nv, ps[nb_i][:, :Lq], ALU.add),
                             reads=[psb[nb_i]], writes=[naccb[j]])
                        K.op(K.dve, lambda e, dv=dv, db_i=db_i, Lq=Lq: e.tensor_tensor(dv, dv, ps[db_i][:, :Lq], ALU.add),
                             reads=[psb[db_i]], writes=[daccb[j]])
                vt += 2
                ui += 1
        obo = slots("obo", 2, c)
        for j in range(4):
            K.op(K.dve, lambda e, j=j: e.reciprocal(dacc[:, j, :], dacc[:, j, :]), reads=[], writes=[daccb[j]])
            ob, oap = obo[j % 2]
            K.op(K.dve, lambda e, j=j, oap=oap: e.tensor_tensor(oap[:, :c], nacc[:, j, :], dacc[:, j, :], ALU.mult),
                 reads=[naccb[j], daccb[j]], writes=[ob])
            K.dma(K.pool, ob_d[j, :, 0:c], oap[:, :c], reads=[ob], pwrites=[obb])

        stage_begin()
        xf = AR.f32(nt * D).rearrange("p (t d) -> p t d", t=nt)
        xfb = [buf(f"xf{i}") for i in range(nt)]
        gam = AR.f32(D)
        bet = AR.f32(D)
        lb = buf("lnp")
        K.dma(K.pool, gam, lnp[2], writes=[lb])
        K.dma(K.pool, bet, lnp[3], pwrites=[lb])
        st = AR.f32(24)
        mv = AR.f32(4)
        sb_ = buf("lnstat")
        ystage = [(buf(f"ys{i}"), AR.f32(512)) for i in range(2)]
        xT = AR.bf(KC * TB).rearrange("p (k t) -> p k t", k=KC)
        xTb = buf("xT")
        xbs = slots("xb", 2, D)
        mT = AR.bf(KC * TB).rearrange("p (k t) -> p k t", k=KC)
        mTb = buf("hT")
        oaT = AR.bf(8 * TB).rearrange("p (k t) -> p k t", k=8)
        obT = AR.bf(4 * TB).rearrange("p (k t) -> p k t", k=4)
        oaTb, obTb = buf("oaT"), buf("obT")
        ga = [(buf(f"ga{i}"), AR.f32(TB)) for i in range(2)]
        gbs = [(buf(f"gbs{i}"), AR.f32(TB)) for i in range(2)]
        t1 = [(buf(f"t1{i}"), AR.f32(TB)) for i in range(2)]
        wfm = slots("wfm", 6, KC * 128, lambda a: a.rearrange("p (k c) -> p k c", k=KC))
        seqw = []
        for _ in range(nblk):
            for dt in range(16):
                seqw += [wg[dt], wg[16 + dt], wa_f[dt].rearrange("p k c -> p (k c)"), wb_f[dt].rearrange("p k c -> p (k c)")]
            seqw += [wom_f[dt] for dt in range(16)]

        class MixStream(WStream):
            pass
        ws = WStream(K, [], wbuf, wfm, bg=False)
        ws.seq = seqw

        def ws_next(kc):
            K_ = ws.K
            ns = len(ws.slots)
            while ws.issued < min(len(ws.seq), ws.cur + ns):
                b_, ap_ = ws.slots[ws.issued % ns]
                src_ = ws.seq[ws.issued]
                if len(src_.shape) == 2:
                    K_.dma(K_.sp, ap_.rearrange("p k c -> p (k c)")[:, 0:src_.shape[1]], src_, reads=[wbuf], writes=[b_])
                else:
                    K_.dma(K_.sp, ap_, src_, reads=[wbuf], writes=[b_])
                ws.issued += 1
            b_, ap_ = ws.slots[ws.cur % ns]
            ws.cur += 1
            return b_, ap_

        class _S:
            def next(self_inner):
                return ws_next(KC)
        for b in range(nblk):
            t0 = b * TB
            r0 = si.row0 + t0
            load_xT(x1_d, buf("x1"), r0, TB, xf, xfb, xT, xTb, xbs)
            K.dma(K.pool, oaT, oa_d[:, :, t0:t0 + TB].rearrange("h p t -> p h t"), reads=[oab], writes=[oaTb])
            K.dma(K.pool, obT, ob_d[:, :, t0:t0 + TB].rearrange("h p t -> p h t"), reads=[obb], writes=[obTb])
            for dt in range(16):
                pa_, pb2_, pA_, pB_ = (0, 1, 2, 3)
                w1b, w1 = ws_next(KC)
                K.mm(psb[0], [(ps[0][:, :TB], w1[:, k, :], xT[:, k, :]) for k in range(KC)], reads=[w1b, xTb])
                w2b, w2 = ws_next(KC)
                K.mm(psb[1], [(ps[1][:, :TB], w2[:, k, :], xT[:, k, :]) for k in range(KC)], reads=[w2b, xTb])
                w3b, w3 = ws_next(8)
                K.mm(psb[2], [(ps[2][:, :TB], w3[:, k, :], oaT[:, k, :]) for k in range(8)], reads=[w3b, oaTb])
                w4b, w4 = ws_next(4)
                K.mm(psb[3], [(ps[3][:, :TB], w4[:, k, :], obT[:, k, :]) for k in range(4)], reads=[w4b, obTb])
                gab, gaa = ga[dt % 2]
                gbb, gba = gbs[dt % 2]
                t1b, t1a = t1[dt % 2]
                K.op(K.act, lambda e, gaa=gaa, dt=dt: e.activation(gaa[:, :TB], ps[0][:, :TB], AF.Sigmoid, bias=vecs_sb[:, 8 + dt:9 + dt]),
                     reads=[psb[0], cb], writes=[gab])
                K.op(K.act, lambda e, gba=gba, dt=dt: e.activation(gba[:, :TB], ps[1][:, :TB], AF.Sigmoid, bias=vecs_sb[:, 24 + dt:25 + dt]),
                     reads=[psb[1], cb], writes=[gbb])
                K.op(K.dve, lambda e, gaa=gaa: e.tensor_tensor(gaa[:, :TB], gaa[:, :TB], ps[2][:, :TB], ALU.mult),
                     reads=[psb[2]], writes=[gab])
                K.op(K.dve, lambda e, gba=gba: e.tensor_tensor(gba[:, :TB], gba[:, :TB], ps[3][:, :TB], ALU.mult),
                     reads=[psb[3]], writes=[gbb])
                K.op(K.pool, lambda e, gaa=gaa, gba=gba, dt=dt: e.tensor_tensor(mT[:, dt, :], gaa[:, :TB], gba[:, :TB], ALU.add),
                     reads=[gab, gbb], pwrites=[mTb])
            out_proj_ln(mT, mTb, KC, _S(), 1.0, TB, xf, xfb, ystage, gam, bet, lb, x2_d, buf("x2"), r0, st, mv, sb_)
            mTb.w = dict(mTb.w)

    x_b = buf("xs")

    def whole():
        for sidx, si in enumerate(seqs):
            if upto == "ffn1":
                ffn_stage(xs, x_b, y_out, buf("y"), si.row0, si.c, si.TB, wi[0], wo[0], 0)
                continue
            ffn_stage(xs, x_b, x1_d, buf("x1"), si.row0, si.c, si.TB, wi[0], wo[0], 0)
            m1_stage(si, sidx)
        if upto in ("ffn1", "m1"):
            return
        vt0 = 0
        kx0 = 0
        for sidx, si in enumerate(seqs):
            m2_stage(si, sidx, vt0, kx0)
            vt0 += 2 * len(dil_units(si.c))
            kx0 += 2 * si.nb
        if upto == "m2":
            return
        for sidx, si in enumerate(seqs):
            ffn_stage(x2_d, buf("x2"), y_out, buf("y"), si.row0, si.c, si.TB, wi[1], wo[1], 4)
    whole()
    K.barrier()
    assert K.check_deadlock(), "symbolic deadlock"

    with nc.Block() as block:
        @block.tensor
        def _(e):
            for f in K.pe.prog:
                f(e)

        @block.scalar
        def _(e):
            for f in K.act.prog:
                f(e)

        @block.vector
        def _(e):
            for f in K.dve.prog:
                f(e)

        @block.gpsimd
        def _(e):
            for f in K.pool.prog:
                f(e)

        @block.sync
        def _(e):
            for f in K.sp.prog:
                f(e)
    es.close()
    return nc, seqs


def _bf16(a):
    import ml_dtypes
    return np.asarray(a, np.float32).astype(ml_dtypes.bfloat16)


def host_tables(seq_lens, rank):
    cos128, sin128, cos64, sin64 = [], [], [], []
    idxv, validv, idxk = [], [], []
    for S in seq_lens:
        c = S // NCORES
        pos = (rank * c + np.arange(c)).astype(np.float32)
        for dim, cl, sl in ((128, cos128, sin128), (64, cos64, sin64)):
            inv = (10000.0 ** (-np.arange(0, dim, 2, dtype=np.float32) / dim)).astype(np.float32)
            ang = pos[None, :] * inv[:, None]
            cl.append(np.concatenate([np.cos(ang), np.cos(ang)], 0).astype(np.float32))
            sl.append(np.concatenate([np.sin(ang), np.sin(ang)], 0).astype(np.float32))
        rows_per_rank_v = RB * c // 512
        for (g, r, l0, Lq) in dil_units(c):
            dil = DILS[g]
            for tile in range(2):
                kidx = np.arange(128)
                l = l0 - HALF + kidx if tile == 0 else l0 + HALF + kidx
                p = rank * c + l * dil + r
                ok = (p >= 0) & (p < S)
                if tile == 1:
                    ok &= kidx < Lq
                rk = np.clip(p, 0, S - 1) // c
                tok = np.clip(p, 0, S - 1) % c
                iv = rk * rows_per_rank_v + tok
                idxv.append(np.where(ok, iv, 0).astype(np.int32))
                validv.append(ok.astype(np.float32))
        nb = -(-1024 // c)
        for off in list(range(-nb, 0)) + list(range(1, nb + 1)):
            rr = rank + off
            d = np.arange(128)
            idxk.append((rr * RB + d).astype(np.int32) if 0 <= rr < NCORES else (rank * RB + d).astype(np.int32))
    t = {
        "cos128": np.concatenate(cos128, 1), "sin128": np.concatenate(sin128, 1),
        "cos64": np.concatenate(cos64, 1), "sin64": np.concatenate(sin64, 1),
        "idxv": np.stack(idxv, 1), "validv": np.stack(validv, 1), "idxk": np.stack(idxk, 1),
    }
    return {k: np.ascontiguousarray(v) for k, v in t.items()}


def const_mats():
    ident = np.eye(128, dtype=np.float32)
    ones = np.ones((128, 128), np.float32)
    r128T = np.zeros((128, 128), np.float32)
    for m in range(64):
        r128T[m + 64, m] = -1.0
        r128T[m, m + 64] = 1.0
    r64T = np.zeros((128, 128), np.float32)
    for m in range(32):
        r64T[m + 32, m] = -1.0
        r64T[m, m + 32] = 1.0
    k = np.arange(128)[:, None]
    q = np.arange(128)[None, :]
    maskA = (k >= q).astype(np.float32)
    maskB = (k <= q).astype(np.float32)
    cm = np.concatenate([ident, ones, r128T, r64T, maskA, maskB], 1)
    return _bf16(cm), np.concatenate([ident, ones], 1).astype(np.float32)


_CACHE = {}


def run(seq_arrays, params, upto=None, sim=False):
    seq_lens = tuple(int(a.shape[0]) for a in seq_arrays)
    if (seq_lens, upto) not in _CACHE:
        _CACHE[(seq_lens, upto)] = build_program(seq_lens, upto, SKIP)
    nc, seqs = _CACHE[(seq_lens, upto)]
    cm, cmf = const_mats()
    f32 = lambda a: np.ascontiguousarray(np.asarray(a, np.float32))
    lnp = np.stack([np.broadcast_to(f32(params[k])[None, :], (128, D)) for k in
                    ("ln1_g", "ln1_b", "ln2_g", "ln2_b", "ln3_g", "ln3_b")], 0).astype(np.float32)
    vecs = np.concatenate([f32(params["q_norm_g"]).reshape(4, 128).T, f32(params["kv_norm_g"]).reshape(4, 128).T,
                           f32(params["b_gate"]).reshape(32, 128).T], 1).astype(np.float32)
    shared = {
        "ffn1_w_in": f32(params["ffn1_w_in"]), "ffn1_w_out": f32(params["ffn1_w_out"]),
        "ffn2_w_in": f32(params["ffn2_w_in"]), "ffn2_w_out": f32(params["ffn2_w_out"]),
        "w_in_mix": f32(params["w_in_mix"]), "w_uq": f32(params["w_uq"]), "w_ukv": f32(params["w_ukv"]),
        "w_branch_a": f32(params["w_branch_a"]), "w_branch_b": f32(params["w_branch_b"]),
        "w_out_mix": f32(params["w_out_mix"]), "lnp": np.ascontiguousarray(lnp), "vecs": np.ascontiguousarray(vecs),
        "cmat": np.ascontiguousarray(cm), "cmatf": np.ascontiguousarray(cmf),
    }
    in_maps = []
    for r in range(NCORES):
        m = dict(shared)
        m["xs"] = np.ascontiguousarray(np.concatenate(
            [f32(a)[r * (a.shape[0] // NCORES):(r + 1) * (a.shape[0] // NCORES)] for a in seq_arrays], 0))
        m.update(host_tables(seq_lens, r))
        in_maps.append(m)
    if sim:
        return nc, in_maps
    res = run_bass_kernel_spmd(nc, in_maps, core_ids=list(range(NCORES)))
    outs = []
    row = 0
    for a in seq_arrays:
        c = a.shape[0] // NCORES
        outs.append(np.concatenate([res.results[r]["y"][row:row + c] for r in range(NCORES)], 0))
        row += c
    return outs


def kernel(x_prompt, x_sample, ffn1_w_in, ffn1_w_out, ln1_g, ln1_b, w_in_mix, b_gate, q_norm_g, w_uq,
           kv_norm_g, w_ukv, w_branch_a, w_branch_b, w_out_mix, ln2_g, ln2_b, ffn2_w_in, ffn2_w_out,
           ln3_g, ln3_b):
    params = dict(ffn1_w_in=ffn1_w_in, ffn1_w_out=ffn1_w_out, ln1_g=ln1_g, ln1_b=ln1_b, w_in_mix=w_in_mix,
                  b_gate=b_gate, q_norm_g=q_norm_g, w_uq=w_uq, kv_norm_g=kv_norm_g, w_ukv=w_ukv,
                  w_branch_a=w_branch_a, w_branch_b=w_branch_b, w_out_mix=w_out_mix, ln2_g=ln2_g, ln2_b=ln2_b,
                  ffn2_w_in=ffn2_w_in, ffn2_w_out=ffn2_w_out, ln3_g=ln3_g, ln3_b=ln3_b)
    params = {k: np.asarray(v)[0] for k, v in params.items()}
    xp = np.asarray(x_prompt, np.float32)
    xsm = np.asarray(x_sample, np.float32)
    seq_arrays = [xsm[0], xsm[1], xp[0], xp[1]]
    outs = run(seq_arrays, params)
    y_sample = np.stack([outs[0], outs[1]], 0).astype(np.float32)
    y_prompt = np.stack([outs[2], outs[3]], 0).astype(np.float32)
    return (y_prompt, y_sample)
```
